# Optimizing a Trainium2 kernel written in Bass

```python
import jax, jax.numpy as jnp
from jax import lax
import numpy as np

D_MODEL = 1024
BATCH = 8
SEQ = 4096
DEPTH = 1
DEC_BATCH = 8
DEC_SEQ = 2048
PAST_LEN = 128

N_HEADS = 8
QK_NOPE = 64
QK_ROPE = 32
V_HEAD = 64
Q_LORA = 384
KV_LORA = 256
ATTN_WIDTH = N_HEADS * V_HEAD
ROPE_THETA = 10000.0
Q_BLOCK = 128
ATTN_SCALE = (QK_NOPE + QK_ROPE) ** -0.5
CONV_CH = D_MODEL - ATTN_WIDTH
CONV_K = 31
CONV_PAD = (CONV_K - 1) // 2
IN_WIDTH = Q_LORA + KV_LORA + QK_ROPE + 2 * CONV_CH
MIX_WIDTH = ATTN_WIDTH + CONV_CH
D_FF = -(-8 * D_MODEL // (3 * 256)) * 256
EPS = 1e-6

kernel_name = "hymba_mla_conformer_encoder"


def rms_norm(x, g):
    xf = x.astype(jnp.float32)
    y = xf * lax.rsqrt(jnp.mean(xf * xf, axis=-1, keepdims=True) + EPS)
    return (y * g.astype(jnp.float32)).astype(x.dtype)


def layer_norm(x, g, b):
    xf = x.astype(jnp.float32)
    mu = jnp.mean(xf, axis=-1, keepdims=True)
    var = jnp.mean(jnp.square(xf - mu), axis=-1, keepdims=True)
    y = (xf - mu) * lax.rsqrt(var + EPS)
    return (y * g.astype(jnp.float32) + b.astype(jnp.float32)).astype(x.dtype)


def rope_tables(seq_len):
    inv = 1.0 / (ROPE_THETA ** (jnp.arange(0, QK_ROPE, 2, dtype=jnp.float32) / QK_ROPE))
    ang = jnp.arange(seq_len, dtype=jnp.float32)[:, None] * inv[None, :]
    return jnp.cos(ang)[None, :, None, :], jnp.sin(ang)[None, :, None, :]


def apply_rope(x, cos, sin):
    xf = x.astype(jnp.float32)
    x1, x2 = jnp.split(xf, 2, axis=-1)
    return jnp.concatenate([x1 * cos - x2 * sin, x2 * cos + x1 * sin], axis=-1).astype(x.dtype)


def block_attention(q, k, v):
    B, S, H, Dq = q.shape
    nb = S // Q_BLOCK
    qb = q.reshape(B, nb, Q_BLOCK, H, Dq).transpose(1, 0, 2, 3, 4)

    def one(qi):
        s = jnp.einsum('bqhd,bkhd->bhqk', qi, k).astype(jnp.float32) * ATTN_SCALE
        p = jax.nn.softmax(s, axis=-1).astype(v.dtype)
        return jnp.einsum('bhqk,bkhd->bqhd', p, v)

    o = lax.map(one, qb)
    return o.transpose(1, 0, 2, 3, 4).reshape(B, S, H * v.shape[-1])


def mixer(h, w_in, q_norm_g, w_uq, kv_norm_g, w_ukv, dw_w, dw_b,
          conv_ln_g, conv_ln_b, attn_out_g, conv_out_g, w_out):
    B, S, _ = h.shape
    proj = h @ w_in
    cut = np.cumsum([Q_LORA, KV_LORA, QK_ROPE, CONV_CH]).tolist()
    cq, ckv, kr, conv_a, conv_g = jnp.split(proj, cut, axis=-1)

    q = (rms_norm(cq, q_norm_g) @ w_uq).reshape(B, S, N_HEADS, QK_NOPE + QK_ROPE)
    q_nope, q_rope = jnp.split(q, [QK_NOPE], axis=-1)
    kv = (rms_norm(ckv, kv_norm_g) @ w_ukv).reshape(B, S, N_HEADS, QK_NOPE + V_HEAD)
    k_nope, v = jnp.split(kv, [QK_NOPE], axis=-1)
    cos, sin = rope_tables(S)
    q_rope = apply_rope(q_rope, cos, sin)
    k_rope = apply_rope(kr[:, :, None, :], cos, sin)
    qf = jnp.concatenate([q_nope, q_rope], axis=-1)
    kf = jnp.concatenate([k_nope, jnp.broadcast_to(k_rope, (B, S, N_HEADS, QK_ROPE))], axis=-1)
    attn = block_attention(qf, kf, v)

    u = conv_a * jax.nn.sigmoid(conv_g)
    u = lax.conv_general_dilated(u, dw_w[:, None, :], window_strides=(1,),
                                 padding=[(CONV_PAD, CONV_PAD)],
                                 dimension_numbers=("NWC", "WIO", "NWC"),
                                 feature_group_count=CONV_CH) + dw_b
    u = jax.nn.silu(layer_norm(u, conv_ln_g, conv_ln_b))

    merged = jnp.concatenate([rms_norm(attn, attn_out_g), rms_norm(u, conv_out_g)], axis=-1)
    return merged @ w_out


def swiglu(h, w_gate, w_up, w_down):
    return (jax.nn.silu(h @ w_gate) * (h @ w_up)) @ w_down


def trunk(x, norm1_g, w_in, q_norm_g, w_uq, kv_norm_g, w_ukv, dw_w, dw_b,
          conv_ln_g, conv_ln_b, attn_out_g, conv_out_g, w_out,
          norm2_g, w_gate, w_up, w_down, final_g):
    for l in range(DEPTH):
        x = x + mixer(rms_norm(x, norm1_g[l]), w_in[l], q_norm_g[l], w_uq[l],
                      kv_norm_g[l], w_ukv[l], dw_w[l], dw_b[l], conv_ln_g[l],
                      conv_ln_b[l], attn_out_g[l], conv_out_g[l], w_out[l])
        x = x + swiglu(rms_norm(x, norm2_g[l]), w_gate[l], w_up[l], w_down[l])
    return rms_norm(x, final_g)


def setup_inputs(seed: int = 0) -> dict:
    key = jax.random.key(seed)
    ks = jax.random.split(key, 24)
    f32 = jnp.float32

    def w(k, shape, fan_in):
        return jax.random.normal(k, shape, f32) * (fan_in ** -0.5)

    def gain(k, shape):
        return 1.0 + 0.01 * jax.random.normal(k, shape, f32)

    def bias(k, shape):
        return 0.01 * jax.random.normal(k, shape, f32)

    L = DEPTH
    return {
        "x_prompt": jax.random.normal(ks[0], (BATCH, SEQ, D_MODEL), f32),
        "x_sample": jax.random.normal(ks[1], (DEC_BATCH, DEC_SEQ, D_MODEL), f32),
        "norm1_g": gain(ks[2], (L, D_MODEL)),
        "w_in": w(ks[3], (L, D_MODEL, IN_WIDTH), D_MODEL),
        "q_norm_g": gain(ks[4], (L, Q_LORA)),
        "w_uq": w(ks[5], (L, Q_LORA, N_HEADS * (QK_NOPE + QK_ROPE)), Q_LORA),
        "kv_norm_g": gain(ks[6], (L, KV_LORA)),
        "w_ukv": w(ks[7], (L, KV_LORA, N_HEADS * (QK_NOPE + V_HEAD)), KV_LORA),
        "dw_w": w(ks[8], (L, CONV_K, CONV_CH), CONV_K),
        "dw_b": bias(ks[9], (L, CONV_CH)),
        "conv_ln_g": gain(ks[10], (L, CONV_CH)),
        "conv_ln_b": bias(ks[11], (L, CONV_CH)),
        "attn_out_g": gain(ks[12], (L, ATTN_WIDTH)),
        "conv_out_g": gain(ks[13], (L, CONV_CH)),
        "w_out": w(ks[14], (L, MIX_WIDTH, D_MODEL), MIX_WIDTH),
        "norm2_g": gain(ks[15], (L, D_MODEL)),
        "w_gate": w(ks[16], (L, D_MODEL, D_FF), D_MODEL),
        "w_up": w(ks[17], (L, D_MODEL, D_FF), D_MODEL),
        "w_down": w(ks[18], (L, D_FF, D_MODEL), D_FF),
        "final_g": gain(ks[19], (D_MODEL,)),
    }


def reference(x_prompt, x_sample, norm1_g, w_in, q_norm_g, w_uq, kv_norm_g, w_ukv,
              dw_w, dw_b, conv_ln_g, conv_ln_b, attn_out_g, conv_out_g, w_out,
              norm2_g, w_gate, w_up, w_down, final_g):
    y_prompt = trunk(x_prompt, norm1_g, w_in, q_norm_g, w_uq, kv_norm_g, w_ukv,
                     dw_w, dw_b, conv_ln_g, conv_ln_b, attn_out_g, conv_out_g,
                     w_out, norm2_g, w_gate, w_up, w_down, final_g)
    y_sample = trunk(x_sample, norm1_g, w_in, q_norm_g, w_uq, kv_norm_g, w_ukv,
                     dw_w, dw_b, conv_ln_g, conv_ln_b, attn_out_g, conv_out_g,
                     w_out, norm2_g, w_gate, w_up, w_down, final_g)
    return (y_prompt, y_sample)
```

```python
import contextlib
import numpy as np
import concourse.bass as bass
import concourse.mybir as mybir
from concourse.bass_utils import run_bass_kernel_spmd

F32 = mybir.dt.float32
BF16 = mybir.dt.bfloat16
AF = mybir.ActivationFunctionType
ALU = mybir.AluOpType

D = 1024
NH = 8
QLORA = 384
KVLORA = 256
INW = 1696
DFF = 2816
NFC = 22
EPS = 1e-6
SCALE = 96 ** -0.5
S_LIST = (4096, 2048)
T = 512
NCORES = 8
DMA_SCRATCH = 256
ALLOC_S = 4096
BG_RATE = 0.55
import os
ACT_COPY = int(os.environ.get('K_ACT_COPY', '0'))
ACT_RECIP = int(os.environ.get('K_ACT_RECIP', '1'))
BG_GU = int(os.environ.get("K_BG_GU", "4"))
BG_DN = int(os.environ.get("K_BG_DN", "2"))
ATT_WINDOWS = {4096: int(os.environ.get("K_ATT0", "2")), 2048: int(os.environ.get("K_ATT1", "1"))}

COMPUTE = ("pe", "act", "dve", "pool")
ENGS = ("pe", "act", "dve", "pool", "sp")
NDMASEM = 8


class Op:
    __slots__ = ("eng", "fn", "deps", "dma", "sig", "sem", "val", "needs_sig")

    def __init__(self, eng, fn, deps, dma):
        self.eng = eng
        self.fn = fn
        self.deps = deps
        self.dma = dma
        self.sig = None
        self.sem = None
        self.val = None
        self.needs_sig = False


class Prog:
    def __init__(self, nc):
        self.nc = nc
        self.ops = []
        self.last_writer = {}
        self.readers = {}
        self.last_op = {}
        self.dma_since = []
        self.bar_deps = set()
        self.bar_pending = set()

    def barrier(self):
        self.bar_deps = set(self.last_op.values()) | set(self.dma_since)
        self.bar_pending = set(ENGS)
        self.dma_since = []

    @staticmethod
    def _expand(keys):
        out = []
        for k in keys:
            if type(k) is tuple and len(k) == 2 and k[0] == "ps":
                out.append(("ps", k[1], 0))
                out.append(("ps", k[1], 1))
            else:
                out.append(k)
        return out

    def op(self, eng, fn, reads=(), writes=(), dma=False):
        reads = self._expand(reads)
        writes = self._expand(writes)
        deps = set()
        lw = self.last_writer
        rd = self.readers
        for k in reads:
            w = lw.get(k)
            if w is not None:
                deps.add(w)
        for k in writes:
            w = lw.get(k)
            if w is not None:
                deps.add(w)
            r = rd.get(k)
            if r:
                deps.update(r.values())
        if eng in self.bar_pending:
            deps |= self.bar_deps
            self.bar_pending.discard(eng)
        ops = self.ops
        idx = len(ops)
        if eng == "pe" and not dma:
            deps = {d for d in deps if ops[d].dma or ops[d].eng != "pe"}
        ops.append(Op(eng, fn, deps, dma))
        rkey = ("dma", idx) if dma else eng
        for k in reads:
            rd.setdefault(k, {})[rkey] = idx
        for k in writes:
            lw[k] = idx
            rd[k] = {}
        if dma:
            self.dma_since.append(idx)
        else:
            self.last_op[eng] = idx
        return idx

    def emit(self):
        nc = self.nc
        ops = self.ops
        for o in ops:
            for d in o.deps:
                ops[d].needs_sig = True
        handles = {"pe": "tensor", "act": "scalar", "dve": "vector", "pool": "gpsimd", "sp": "sync"}
        cnt = {e: 0 for e in ENGS}
        dcnt = {}
        dma_rr = {e: 0 for e in ENGS}
        per_eng = {e: [] for e in ENGS}
        for o in ops:
            per_eng[o.eng].append(o)
            if o.dma:
                s = dma_rr[o.eng] % NDMASEM
                dma_rr[o.eng] += 1
                key = (o.eng, s)
                dcnt[key] = dcnt.get(key, 0) + 16
                o.sem = key
                o.val = dcnt[key]
            elif o.needs_sig:
                cnt[o.eng] += 1
                o.sig = cnt[o.eng]
        with contextlib.ExitStack() as st:
            sems = {}
            for e in COMPUTE:
                sems[e] = st.enter_context(nc.semaphore("s_" + e))
            for e in ENGS:
                for s in range(min(NDMASEM, dma_rr[e])):
                    sems[(e, s)] = st.enter_context(nc.semaphore("d_%s_%d" % (e, s)))
            block = st.enter_context(nc.Block())

            def make(e):
                def body(eng):
                    wm = {}
                    for o in per_eng[e]:
                        need = {}
                        for d in o.deps:
                            p = ops[d]
                            if p.dma:
                                k, v = p.sem, p.val
                            else:
                                k, v = p.eng, p.sig
                            if need.get(k, 0) < v:
                                need[k] = v
                        if o.dma:
                            k, v = o.sem, o.val - 16
                            if v > 0 and need.get(k, 0) < v:
                                need[k] = v
                        for k, v in need.items():
                            if wm.get(k, 0) < v:
                                eng.wait_ge(sems[k], v)
                                wm[k] = v
                        ins = o.fn(eng)
                        if o.dma:
                            ins.then_inc(sems[o.sem], 16)
                        elif o.sig is not None:
                            ins.then_inc(sems[o.eng], 1)
                    if e == "sp":
                        for (k, v) in dcnt.items():
                            eng.wait_ge(sems[k], v)
                return body

            for e in ENGS:
                if per_eng[e] or e == "sp":
                    getattr(block, handles[e])(make(e))


KB = 1024


class Builder:
    def __init__(self):
        nc = bass.Bass("TRN2", target_bir_lowering=False, dynamic_dma_scratch_size=DMA_SCRATCH)
        self.nc = nc
        self.P = Prog(nc)
        self.uid = 0
        self.rr = 0
        dt = nc.dram_tensor
        self.xin = [dt("xp", [S_LIST[0], D], F32, kind="ExternalInput"),
                    dt("xs", [S_LIST[1], D], F32, kind="ExternalInput")]
        self.yout = [dt("yp", [S_LIST[0], D], F32, kind="ExternalOutput"),
                     dt("ys", [S_LIST[1], D], F32, kind="ExternalOutput")]
        self.w = {}
        for name, shape in [("norm1_g", [D]), ("w_in", [D, INW]), ("q_norm_g", [QLORA]),
                            ("w_uq", [QLORA, 768]), ("kv_norm_g", [KVLORA]), ("w_ukv", [KVLORA, 1024]),
                            ("dw_w", [31, 512]), ("dw_b", [512]), ("conv_ln_g", [512]),
                            ("conv_ln_b", [512]), ("attn_out_g", [512]), ("conv_out_g", [512]),
                            ("w_out", [D, D]), ("norm2_g", [D]), ("w_gate", [D, DFF]),
                            ("w_up", [D, DFF]), ("w_down", [DFF, D]), ("final_g", [D]),
                            ("rope_cs", [128, S_LIST[0]])]:
            self.w[name] = dt(name, shape, F32, kind="ExternalInput")
        self.s_win = dt("s_win", [128, 8, 1792], BF16, kind="Internal")
        self.s_wuq = dt("s_wuq", [128, 3, 1024], BF16, kind="Internal")
        self.s_wukv = dt("s_wukv", [128, 2, 1024], BF16, kind="Internal")
        self.s_wout = dt("s_wout", [128, 8, 1024], BF16, kind="Internal")
        self.s_gu = dt("s_gu", [11, 128, 8, 512], BF16, kind="Internal")
        self.s_wdown = dt("s_wdown", [128, NFC, 1024], BF16, kind="Internal")
        self.ps2 = [nc.alloc_psum_tensor("psd%d" % i, [128, 1024], F32) for i in range(4)]
        self.ps = [self.ps2[i // 2][:, (i % 2) * 512:(i % 2) * 512 + 512] for i in range(8)]
        self.psb = [self.ps2[i // 2].bitcast(BF16)[:, (i % 2) * 1024:(i % 2) * 1024 + 1024] for i in range(8)]
        self.bank_rr = 0

    def at(self, name, shape, dtype, off):
        self.uid += 1
        t = self.nc.alloc_sbuf_tensor_at("%s_%d" % (name, self.uid), list(shape), dtype, offset=off)
        n = 1
        for s in shape[1:]:
            n *= s
        size = n * (4 if dtype == F32 else 2)
        size = (size + 63) // 64 * 64
        assert off + size <= 229344, (name, off, size)
        return t, off + size

    def dma(self, out, in_, reads=(), writes=(), slow=False):
        if slow:
            fn = lambda e: e.dma_start(out=out, in_=in_, allow_slow_non_contiguous=True)
        else:
            fn = lambda e: e.dma_start(out=out, in_=in_)
        self.P.op("sp", fn, reads=reads, writes=writes, dma=True)

    def mm(self, out, lhsT, rhs, start, stop, reads, writes):
        self.P.op("pe", lambda e: e.matmul(out, lhsT=lhsT, rhs=rhs, start=start, stop=stop),
                  reads=reads, writes=writes)

    def tr(self, out, in_, ident, reads, writes):
        self.P.op("pe", lambda e: e.transpose(out=out, in_=in_, identity=ident), reads=reads, writes=writes)

    def act(self, out, in_, func, reads, writes, scale=None, bias=None, accum=None):
        kw = {}
        if scale is not None:
            kw["scale"] = scale
        if bias is not None:
            kw["bias"] = bias
        if accum is not None:
            kw["accum_out"] = accum
        self.P.op("act", lambda e: e.activation(out=out, in_=in_, func=func, **kw), reads=reads, writes=writes)

    def ts(self, eng, out, in0, s1, reads, writes, op0=ALU.mult, s2=None, op1=None):
        if op1 is None and eng == "pool" and op0 == ALU.mult:
            fn = lambda e: e.tensor_scalar(out=out, in0=in0, scalar1=s1, scalar2=1.0, op0=ALU.mult, op1=ALU.mult)
        elif op1 is None:
            fn = lambda e: e.tensor_scalar(out=out, in0=in0, scalar1=s1, scalar2=None, op0=op0)
        else:
            fn = lambda e: e.tensor_scalar(out=out, in0=in0, scalar1=s1, scalar2=s2, op0=op0, op1=op1)
        self.P.op(eng, fn, reads=reads, writes=writes)

    def tt(self, eng, out, in0, in1, op, reads, writes):
        self.P.op(eng, lambda e: e.tensor_tensor(out=out, in0=in0, in1=in1, op=op), reads=reads, writes=writes)

    def stt(self, out, in0, scalar, in1, op0, op1, reads, writes):
        self.P.op("dve", lambda e: e.scalar_tensor_tensor(out=out, in0=in0, scalar=scalar, in1=in1, op0=op0, op1=op1),
                  reads=reads, writes=writes)

    def copy(self, eng, out, in_, reads, writes):
        if eng == "act":
            self.P.op("act", lambda e: e.activation(out=out, in_=in_, func=AF.Copy), reads=reads, writes=writes)
        else:
            self.P.op(eng, lambda e: e.tensor_copy(out=out, in_=in_), reads=reads, writes=writes)

    def recip(self, out, in_, reads, writes):
        self.P.op("dve", lambda e: e.reciprocal(out=out, in_=in_), reads=reads, writes=writes)

    def memset(self, eng, ap, val, writes):
        self.P.op(eng, lambda e: e.memset(ap, val), writes=writes)

    def scale_cast(self, out, in_, sc, reads, writes):
        e = ("dve", "act")[self.rr % 2]
        self.rr += 1
        if e == "act":
            self.act(out, in_, AF.Copy, reads, writes, scale=sc)
        else:
            self.ts(e, out, in_, sc, reads, writes)

    def bank(self):
        b = self.bank_rr % 8
        self.bank_rr += 1
        return b

    def consts(self):
        P = self.P
        off = (self.nc.sbuf_base + 63) // 64 * 64
        self.identf, off = self.at("identf", [128, 128], F32, off)
        self.ident, off = self.at("ident", [128, 128], BF16, off)
        self.onesf, off = self.at("onesf", [128, 128], F32, off)
        self.vec, off = self.at("vec", [128, 64], F32, off)
        self.dww, off = self.at("dww", [128, 4, 32], F32, off)
        self.gfbc, off = self.at("gfbc", [128, D], F32, off)
        self.junk, off = self.at("junk", [128, D], BF16, off)
        self.stat, off = self.at("stat", [128, 64], F32, off)
        self.dwtmp, off = self.at("dwtmp", [31, 512], F32, off)
        self.const_end = off
        identf = self.identf
        self.memset("pool", identf[:], 0.0, ["identf"])
        P.op("pool", lambda e: e.affine_select(out=identf[:], in_=identf[:], pattern=[[-1, 128]],
                                               compare_op=ALU.not_equal, fill=1.0, base=0, channel_multiplier=1),
             reads=["identf"], writes=["identf"])
        self.copy("dve", self.ident[:], identf[:], ["identf"], ["ident"])
        self.memset("dve", self.onesf[:], 1.0, ["onesf"])
        w = self.w
        vec = self.vec

        def ld(col, n, name, o=0):
            self.dma(vec[:, col:col + n], w[name].ap().rearrange("(k p) -> p k", p=128), writes=[("vec", name)], slow=True)
        ld(0, 8, "norm1_g")
        ld(16, 3, "q_norm_g")
        ld(22, 2, "kv_norm_g")
        ld(24, 4, "attn_out_g")
        ld(28, 4, "conv_out_g")
        ld(32, 8, "norm2_g")
        ld(40, 4, "dw_b")
        ld(44, 4, "conv_ln_g")
        ld(48, 4, "conv_ln_b")
        self.memset("dve", vec[:, 52:53], EPS, [("vec", "eps")])
        self.ts("dve", vec[:, 8:16], vec[:, 0:8], -1.0, [("vec", "norm1_g")], [("vec", "ng1")])
        self.ts("dve", vec[:, 19:22], vec[:, 16:19], -1.0, [("vec", "q_norm_g")], [("vec", "ngq")])
        self.dma(self.gfbc[:], w["final_g"].ap().partition_broadcast(128), writes=["gfbc"])
        tmp = self.dwtmp
        self.dma(tmp[:], w["dw_w"][:, :], writes=["dwtmp"])
        for c in range(4):
            self.P.op("pe", lambda e, c=c: e.transpose(out=self.ps[c][:, 0:31], in_=tmp[:, c * 128:(c + 1) * 128],
                                                       identity=identf[0:31, 0:31]),
                      reads=["dwtmp", "identf"], writes=[("ps", c, 0), ("ps", c, 1)])
            self.copy("dve", self.dww[:, c, 0:31], self.ps[c][:, 0:31], [("ps", c, 0), ("ps", c, 1)], ["dww"])

    def prologue(self):
        w = self.w
        vec = self.vec
        base = self.const_end
        off = base
        sf = []
        NSF = 6
        NSB = 3
        for i in range(NSF):
            t, off = self.at("sf%d" % i, [128, 2816], F32, off)
            sf.append(t)
        sb = []
        for i in range(NSB):
            t, off = self.at("sb%d" % i, [128, 11, 512], BF16, off)
            sb.append(t)
        big_f, off = self.at("bigf", [128, 8, 1024], F32, off)
        big_b, off = self.at("bigb", [128, 8, 1024], BF16, off)
        self.Y_OFF = self.const_end + 166528
        assert off <= self.Y_OFF, off
        winb, _ = self.at("winb", [128, 8, 1792], BF16, self.Y_OFF)
        self.win0 = winb

        for kc in range(8):
            s = sf[kc % NSF]
            rk = ("sf", kc % NSF)
            self.dma(s[:, 0:INW], w["w_in"][kc * 128:(kc + 1) * 128, :], writes=[rk])
            g = vec[:, kc:kc + 1]
            ng = vec[:, 8 + kc:9 + kc]
            wk = [("winb", kc)]
            self.scale_cast(winb[:, kc, 0:640], s[:, 0:640], g, [rk, ("vec", "norm1_g")], wk)
            self.memset("pool", winb[:, kc, 640:704], 0.0, wk)
            self.scale_cast(winb[:, kc, 704:736], s[:, 640:672], g, [rk, ("vec", "norm1_g")], wk)
            self.ts("dve", winb[:, kc, 736:752], s[:, 656:672], ng, [rk, ("vec", "ng1")], wk)
            self.scale_cast(winb[:, kc, 752:768], s[:, 640:656], g, [rk, ("vec", "norm1_g")], wk)
            self.scale_cast(winb[:, kc, 768:1792], s[:, 672:1696], g, [rk, ("vec", "norm1_g")], wk)
        self.dma(self.s_win[:, :, :], winb[:], reads=[("winb", kc) for kc in range(8)], writes=["s_win"])

        uq_f = big_f[:, 0:3, 0:768]
        self.dma(uq_f, w["w_uq"].ap().rearrange("(k p) c -> p k c", p=128), writes=["bigf"])
        for kc in range(3):
            g = vec[:, 16 + kc:17 + kc]
            ng = vec[:, 19 + kc:20 + kc]
            src = big_f[:, kc, 0:768].rearrange("p (h c) -> p h c", c=96)
            dst = big_b[:, kc, :].rearrange("p (h c) -> p h c", c=128)
            rd = ["bigf", ("vec", "q_norm_g"), ("vec", "ngq")]
            self.ts("dve", dst[:, :, 0:96], src[:, :, 0:96], g, rd, [("bigb", kc)])
            self.ts("dve", dst[:, :, 96:112], src[:, :, 80:96], ng, rd, [("bigb", kc)])
            self.ts("dve", dst[:, :, 112:128], src[:, :, 64:80], g, rd, [("bigb", kc)])
        self.dma(self.s_wuq[:, :, :], big_b[:, 0:3, :], reads=[("bigb", k) for k in range(3)], writes=["s_wuq"])

        self.dma(big_f[:, 4:6, :], w["w_ukv"].ap().rearrange("(k p) c -> p k c", p=128), writes=["bigf2"])
        for kc in range(2):
            self.scale_cast(big_b[:, 4 + kc, :], big_f[:, 4 + kc, :], vec[:, 22 + kc:23 + kc],
                            ["bigf2", ("vec", "kv_norm_g")], [("bigb", 4 + kc)])
        self.dma(self.s_wukv[:, :, :], big_b[:, 4:6, :], reads=[("bigb", 4), ("bigb", 5)], writes=["s_wukv"])

        self.P.barrier()

    def ffn_convert_items(self, stage_off):
        w = self.w
        vec = self.vec
        off = stage_off
        fs = []
        for k in range(3):
            t, off = self.at("cvf%d" % k, [128, 1024], F32, off)
            fs.append(t)
        bs = []
        for k in range(2):
            t, off = self.at("cvb%d" % k, [128, 1024], BF16, off)
            bs.append(t)
        assert off <= stage_off + 16384
        pieces = []
        for kc in range(8):
            for (c0, wd) in [(0, 768), (768, 768), (1536, 768), (2304, 512)]:
                for which, name in ((0, "w_gate"), (1, "w_up")):
                    src = w[name][kc * 128:(kc + 1) * 128, c0:c0 + wd]
                    dst = self.s_gu[c0 // 256:(c0 + wd) // 256, :, kc, which * 256:(which + 1) * 256].rearrange("q p c -> p q c")
                    pieces.append((src, wd, vec[:, 32 + kc:33 + kc], dst, "s_gu", 256))
        for kc in range(8):
            pieces.append((w["w_out"][kc * 128:(kc + 1) * 128, :], 1024, vec[:, 24 + kc:25 + kc], self.s_wout[:, kc, :],
                           "s_wout", 1024))
        for fc in range(NFC):
            src = w["w_down"][fc * 128:(fc + 1) * 128, :]
            dst = self.s_wdown[:, fc, :]
            pieces.append((src, 1024, None, dst, "s_wdown", 1024))
        NP = len(pieces)
        R = "cvt_region"
        items = []
        for k in range(NP + 2):
            def f(k=k):
                if k < NP:
                    src, wd, g, dst, dk, cw = pieces[k]
                    self.dma(fs[k % 3][:, 0:wd], src, reads=[R], writes=[("cvf", k % 3)])
                if 1 <= k <= NP:
                    j = k - 1
                    src, wd, g, dst, dk, cw = pieces[j]
                    if g is None:
                        self.copy("dve", bs[j % 2][:, 0:wd], fs[j % 3][:, 0:wd], [("cvf", j % 3), R], [("cvb", j % 2)])
                    else:
                        self.ts("dve", bs[j % 2][:, 0:wd], fs[j % 3][:, 0:wd], g, [("cvf", j % 3), R, ("vec", "norm2_g"), ("vec", "attn_out_g"), ("vec", "conv_out_g")],
                                [("cvb", j % 2)])
                if k >= 2:
                    j = k - 2
                    src, wd, g, dst, dk, cw = pieces[j]
                    sv = bs[j % 2][:, 0:wd]
                    if cw != wd:
                        sv = sv.rearrange("p (q c) -> p q c", c=cw)
                    self.dma(dst, sv, reads=[("cvb", j % 2), R], writes=[dk])
            items.append(f)
        return items

    def sequence(self, si):
        P = self.P
        S = S_LIST[si]
        NT = S // T
        NKC = S // 128
        x = self.xin[si]
        y = self.yout[si]
        ps = self.ps
        psb = self.psb
        vec = self.vec
        ident = self.ident
        onesf = self.onesf
        SMAX = ALLOC_S
        SM = SMAX + 80

        off = self.const_end
        unT, off = self.at("unT", [128, 4, SM], BF16, off)
        atT_off = off
        atT, off = self.at("atT", [128, 4, SMAX], BF16, off)
        G = off
        cqT, off = self.at("cqT", [128, 3, SMAX], BF16, off)
        ckvT, off = self.at("ckvT", [128, 2, SMAX], BF16, off)
        kT = []
        for i in range(2):
            t, off = self.at("kT%d" % i, [128, SMAX], BF16, off)
            kT.append(t)
        cs, off = self.at("cs", [128, SMAX], F32, off)
        offV = off
        vh = []
        for i in range(2):
            t, off = self.at("vh%d" % i, [128, SMAX // 128, 128], BF16, off)
            vh.append(t)
        offW = off
        wuq, off = self.at("wuq", [128, 3, 1024], BF16, off)
        wukv, off = self.at("wukv", [128, 2, 1024], BF16, off)
        Y = off
        if si == 0:
            assert Y == self.Y_OFF, (Y, self.Y_OFF)
            win = self.win0
            offA = Y + 8 * 1792 * 2
        else:
            win, offA = self.at("win", [128, 8, 1792], BF16, Y)
        xa = []
        o2 = atT_off
        for i in range(2):
            t, o2 = self.at("xa%d" % i, [128, 4, D], F32, o2)
            xa.append(t)
        xn, offA = self.at("xn", [128, 4, D], BF16, offA)
        hTa = [None, None]
        hTa[0], offA = self.at("hTa0", [128, 8, T], BF16, offA)
        sqb, offA = self.at("sqb", [128, 2, T], F32, offA)
        rscrA, offA = self.at("rscrA", [128, T], F32, offA)
        sqa, offA2 = self.at("sqa", [128, 3, T], F32, offV)
        rbc, offA2 = self.at("rbc", [128, 2, T], F32, offA2)
        sgl, offA2 = self.at("sgl", [128, 2, T], F32, offA2)
        assert offA2 <= offW
        hTa[1], offA3 = self.at("hTa1", [128, 8, T], BF16, offW)
        kr2, offA3 = self.at("kr2", [128, T], F32, offA3)
        assert offA3 <= Y
        offB = Y
        qT = []
        for i in range(3):
            t, offB = self.at("qT%d" % i, [128, T], BF16, offB)
            qT.append(t)
        pT = []
        for i in range(6):
            t, offB = self.at("pT%d" % i, [128, T], BF16, offB)
            pT.append(t)
        qtmp = []
        for i in range(2):
            t, offB = self.at("qtmp%d" % i, [128, T], F32, offB)
            qtmp.append(t)
        qtmp2 = []
        for i in range(2):
            t, offB = self.at("qtmpb%d" % i, [128, T], F32, offB)
            qtmp2.append(t)
        rc = []
        for i in range(2):
            t, offB = self.at("rc%d" % i, [128, T], F32, offB)
            rc.append(t)
        CTOP = 229344 - 28672
        assert offB <= CTOP
        offC = CTOP
        cacc = []
        for i in range(2):
            t, offC = self.at("cacc%d" % i, [128, 4, T], F32, offC)
            cacc.append(t)
        csq, offC = self.at("csq", [128, 2, T], F32, offC)
        lnm, offC = self.at("lnm", [128, T], F32, offC)
        lnr, offC = self.at("lnr", [128, T], F32, offC)
        lnt, offC = self.at("lnt", [128, 2, T], F32, offC)

        def recip2(out, in_, scratch, reads, writes, kscr):
            self.recip(out, in_, reads, writes)

        if si != 0:
            self.dma(win[:], self.s_win[:, :, :], reads=["s_win"], writes=["win"])
        self.dma(cs[:, 0:S], self.w["rope_cs"][:, 0:S], writes=["cs"])
        for c in range(4):
            self.memset("pool", unT[:, c, 32:48], 0.0, [("u0", c, -1)])
            self.memset("pool", unT[:, c, 48 + S:64 + S], 0.0, [("u0", c, NT)])
        stat = self.stat
        epsb = vec[:, 52:53]
        rrA = [0]

        def nbA():
            b = 2 + rrA[0] % 6
            rrA[0] += 1
            return b

        def front(i):
            xt = xa[i % 2]
            kx = ("xa", i % 2)
            hb = hTa[i % 2]
            so = (i % 2) * 12
            self.dma(xt[:], x[i * T:(i + 1) * T, :].rearrange("(j p) d -> p j d", p=128), writes=[kx])
            for j in range(4):
                self.act(self.junk[:], xt[:, j, :], AF.Square, [kx], ["junk", ("stat", "a", i % 2, j)],
                         accum=stat[:, so + j:so + j + 1])
            self.act(stat[:, so + 4:so + 8], stat[:, so:so + 4], AF.Sqrt, [("stat", "a", i % 2, j) for j in range(4)],
                     [("stat", "a2", i % 2)], scale=1.0 / D, bias=EPS)
            self.recip(stat[:, so + 8:so + 12], stat[:, so + 4:so + 8], [("stat", "a2", i % 2)], [("stat", "a3", i % 2)])
            for j in range(4):
                sc = stat[:, so + 8 + j:so + 9 + j]
                if j % 2 == 0:
                    self.ts("dve", xn[:, j, :], xt[:, j, :], sc, [kx, ("stat", "a3", i % 2)], [("xn", j)])
                else:
                    self.act(xn[:, j, :], xt[:, j, :], AF.Copy, [kx, ("stat", "a3", i % 2)], [("xn", j)], scale=sc)

        def front_b(i):
            hb = hTa[i % 2]
            for c in range(8):
                b = c % 2
                pv = psb[b][:, 0:512]
                kb = ("ps", b)
                for j in range(4):
                    self.tr(pv[:, j * 128:(j + 1) * 128], xn[:, j, c * 128:(c + 1) * 128], ident[:],
                            [("xn", j), "ident"], [kb])
                self.copy("act" if c % 2 == 0 else "dve", hb[:, c, :], pv, [kb], [("hTa", i % 2, c)])

        def back(i):
            if i + 1 < NT:
                front(i + 1)
            hb = hTa[i % 2]

            def proj(chunk, bank):
                for kc in range(8):
                    self.mm(ps[bank][:, :], win[:, kc, chunk * 128:(chunk + 1) * 128], hb[:, kc, :],
                            kc == 0, kc == 7, ["win", ("hTa", i % 2, kc)], [("ps", bank)])

            bq = [nbA() for _ in range(3)]
            for c in range(3):
                proj(c, bq[c])
                self.act(sqa[:, c, :], ps[bq[c]][:, :], AF.Square, [("ps", bq[c])], [("sqa", c)])
            bkv = [nbA() for _ in range(2)]
            for c in range(2):
                proj(3 + c, bkv[c])
                self.act(sqb[:, c, :], ps[bkv[c]][:, :], AF.Square, [("ps", bkv[c])], [("sqb", c)])
            bs = 0
            for c in range(3):
                self.mm(ps[bs][:, :], onesf[:], sqa[:, c, :], c == 0, c == 2, ["onesf", ("sqa", c)], [("ps", bs)])
            self.act(rscrA[:], ps[bs][:, :], AF.Ln, [("ps", bs)], ["rscrA"], scale=1.0 / QLORA, bias=epsb[:, 0:1])
            self.act(rbc[:, 0, :], rscrA[:], AF.Exp, ["rscrA"], [("rbc", 0)], scale=-0.5)
            for c in range(3):
                self.tt("dve", cqT[:, c, i * T:(i + 1) * T], ps[bq[c]][:, :], rbc[:, 0, :], ALU.mult,
                        [("ps", bq[c]), ("rbc", 0)], [("cqT", c, i)])
            bs = 1
            for c in range(2):
                self.mm(ps[bs][:, :], onesf[:], sqb[:, c, :], c == 0, c == 1, ["onesf", ("sqb", c)], [("ps", bs)])
            self.act(rscrA[:], ps[bs][:, :], AF.Ln, [("ps", bs)], ["rscrA"], scale=1.0 / KVLORA, bias=epsb[:, 0:1])
            self.act(rbc[:, 1, :], rscrA[:], AF.Exp, ["rscrA"], [("rbc", 1)], scale=-0.5)
            for c in range(2):
                self.tt("dve", ckvT[:, c, i * T:(i + 1) * T], ps[bkv[c]][:, :], rbc[:, 1, :], ALU.mult,
                        [("ps", bkv[c]), ("rbc", 1)], [("ckvT", c, i)])
            bk = nbA()
            proj(5, bk)
            self.tt("dve", sgl[64:96, 0, :], ps[bk][64:96, :], cs[64:96, i * T:(i + 1) * T], ALU.mult,
                    [("ps", bk), "cs"], [("sgl", 0)])
            self.tt("dve", kr2[64:96, :], ps[bk][96:128, :], cs[96:128, i * T:(i + 1) * T], ALU.mult,
                    [("ps", bk), "cs"], ["kr2"])
            for b in range(2):
                self.tt("pool", kT[b][64:96, i * T:(i + 1) * T], sgl[64:96, 0, :], kr2[64:96, :], ALU.add,
                        [("sgl", 0), "kr2"], [("kTr", b, i)])
            if i + 1 < NT:
                front_b(i + 1)
            for c in range(4):
                ba = nbA()
                bg = nbA()
                proj(6 + c, ba)
                proj(10 + c, bg)
                self.act(sgl[:, 1, :], ps[bg][:, :], AF.Sigmoid, [("ps", bg)], [("sgl", 1)])
                self.tt("dve", unT[:, c, 48 + i * T:48 + (i + 1) * T], ps[ba][:, :], sgl[:, 1, :], ALU.mult,
                        [("ps", ba), ("sgl", 1)], [("u0", c, i)])

        front(0)
        front_b(0)
        for i in range(NT):
            back(i)
        P.barrier()

        dww = self.dww
        self.memset("pool", vh[0][:, :, 64:128], 1.0, [("vh1", 0)])
        self.memset("pool", vh[1][:, :, 0:64], 1.0, [("vh1", 1)])
        self.dma(wuq[:], self.s_wuq[:, :, :], reads=["s_wuq"], writes=["wuq"])
        self.dma(wukv[:], self.s_wukv[:, :, :], reads=["s_wukv"], writes=["wukv"])

        def conv_taps(wdw):
            ca = cacc[wdw % 2]
            t0 = 48 + wdw * T - 15
            items = []
            for j in range(31):
                for c in range(4):
                    def f(j=j, c=c):
                        rk = [("u0", c, wdw - 1), ("u0", c, wdw), ("u0", c, wdw + 1), "dww"]
                        src = unT[:, c, t0 + j:t0 + j + T]
                        kc_ = ("cacc", wdw % 2, c)
                        if j == 0:
                            self.ts("dve", ca[:, c, :], src, dww[:, c, 0:1], rk + [("vec", "dw_b")], [kc_],
                                    op0=ALU.mult, s2=vec[:, 40 + c:41 + c], op1=ALU.add)
                        else:
                            self.stt(ca[:, c, :], src, dww[:, c, j:j + 1], ca[:, c, :], ALU.mult, ALU.add,
                                     rk + [kc_], [kc_])
                    items.append(f)
            return items

        def conv_ln(wdw):
            ca = cacc[wdw % 2]
            ck = lambda c: ("cacc", wdw % 2, c)
            items = []

            def s1():
                for c in range(4):
                    self.mm(ps[7][:, :], onesf[:], ca[:, c, :], c == 0, c == 3, ["onesf", ck(c)], [("ps", 7)])
                self.act(lnm[:], ps[7][:, :], AF.Copy, [("ps", 7)], ["lnm"], scale=1.0 / 512)
            items.append(s1)

            def sqm01():
                for c in range(2):
                    self.act(csq[:, c, :], ca[:, c, :], AF.Square, [ck(c)], [("csq", c)])
                for c in range(2):
                    self.mm(ps[7][:, :], onesf[:], csq[:, c, :], c == 0, False, ["onesf", ("csq", c)], [("ps", 7)])
                self.tt("pool", lnt[:, 0, :], lnm[:], lnm[:], ALU.mult, ["lnm"], [("lnt", 0)])
            items.append(sqm01)

            def sqm23():
                for c in range(2, 4):
                    self.act(csq[:, c % 2, :], ca[:, c, :], AF.Square, [ck(c)], [("csq", c % 2)])
                for c in range(2, 4):
                    self.mm(ps[7][:, :], onesf[:], csq[:, c % 2, :], False, c == 3, ["onesf", ("csq", c % 2)], [("ps", 7)])
            items.append(sqm23)

            def var():
                self.stt(lnr[:], ps[7][:, :], 1.0 / 512, lnt[:, 0, :], ALU.mult, ALU.subtract, [("ps", 7), ("lnt", 0)], ["lnr"])
            items.append(var)

            def sd():
                self.act(lnr[:], lnr[:], AF.Ln, ["lnr"], ["lnr"], bias=epsb[:, 0:1])
            items.append(sd)

            def rcp():
                self.act(lnr[:], lnr[:], AF.Exp, ["lnr"], ["lnr"], scale=-0.5)
            items.append(rcp)
            for c in range(4):
                def sub(c=c):
                    self.tt("pool", lnt[:, c % 2, :], ca[:, c, :], lnm[:], ALU.subtract, [ck(c), "lnm"], [("lnt", c % 2)])
                items.append(sub)

                def mul(c=c):
                    tb = lnt[:, c % 2, :]
                    self.tt("dve", tb, tb, lnr[:], ALU.mult, [("lnt", c % 2), "lnr"], [("lnt", c % 2)])
                items.append(mul)

                def silu(c=c):
                    self.act(unT[:, c, 16 + wdw * T:16 + (wdw + 1) * T], lnt[:, c % 2, :], AF.Silu,
                             [("lnt", c % 2), ("vec", "conv_ln_g"), ("vec", "conv_ln_b")],
                             [("un", c, wdw), ("u0", c, wdw - 1), ("u0", c, wdw)],
                             scale=vec[:, 44 + c:45 + c], bias=vec[:, 48 + c:49 + c])
                items.append(silu)
            return items

        bg = []
        ln_end = {}
        for wdw in range(NT + 1):
            taps = conv_taps(wdw) if wdw < NT else []
            ln = conv_ln(wdw - 1) if wdw >= 1 else []
            if not taps:
                bg.extend(ln)
                ln_end[wdw - 1] = len(bg)
                continue
            step = 2
            li = 0
            for ti, tp in enumerate(taps):
                bg.append(tp)
                if ln and (ti + 1) % step == 0 and li < len(ln):
                    bg.append(ln[li])
                    li += 1
            bg.extend(ln[li:])
            if wdw >= 1:
                ln_end[wdw - 1] = len(bg)
        bgpos = [0]

        def pull_until(pos):
            while bgpos[0] < min(pos, len(bg)):
                bg[bgpos[0]]()
                bgpos[0] += 1

        def pull(n):
            while n > 0 and bgpos[0] < len(bg):
                bg[bgpos[0]]()
                bgpos[0] += 1
                n -= 1

        def kvprep_items(h):
            b = h % 2
            vcol = 0 if b == 0 else 64
            items = []
            bk = 6
            for i in range(NT):
                def fk(i=i):
                    for kc in range(2):
                        self.mm(ps[bk][0:64, :], wukv[:, kc, h * 128:h * 128 + 64], ckvT[:, kc, i * T:(i + 1) * T],
                                kc == 0, kc == 1, ["wukv", ("ckvT", kc, i)], [("ps", bk)])
                    self.copy("act" if (S <= 2048 and ACT_COPY) else "dve", kT[b][0:64, i * T:(i + 1) * T], ps[bk][0:64, :],
                              [("ps", bk)], [("kTn", b, i)])
                items.append(fk)
            for g8 in range(NKC // 8):
                def fv(g8=g8):
                    for t8 in range(8):
                        t = g8 * 8 + t8
                        for kc in range(2):
                            self.mm(ps[bk][:, t8 * 64:(t8 + 1) * 64], ckvT[:, kc, t * 128:(t + 1) * 128],
                                    wukv[:, kc, h * 128 + 64:h * 128 + 128], kc == 0, kc == 1,
                                    ["wukv", ("ckvT", kc, t // 4)], [("ps", bk)])
                    self.copy("dve", vh[b][:, g8 * 8:(g8 + 1) * 8, vcol:vcol + 64],
                              ps[bk][:, :].rearrange("p (a c) -> p a c", c=64), [("ps", bk)], [("vh", b, g8)])
                items.append(fv)
            return items

        def kprep(h):
            pass

        def vprep(h):
            for f in kvprep_items(h):
                f()

        units = [(h, i) for h in range(NH) for i in range(NT)]

        def qprep(u):
            h, i = units[u]
            bk = 6
            qb = qT[u % 3]
            for kc in range(3):
                self.mm(ps[bk][:, :], wuq[:, kc, h * 128:(h + 1) * 128], cqT[:, kc, i * T:(i + 1) * T],
                        kc == 0, kc == 2, ["wuq", ("cqT", kc, i)], [("ps", bk)])
            self.copy("act" if (S <= 2048 and ACT_COPY) else "dve", qb[0:64, :], ps[bk][0:64, :], [("ps", bk)], [("qTn", u % 3)])
            qt = qtmp[u % 2]
            self.tt("dve", qt[64:96, :], ps[bk][64:96, :], cs[64:96, i * T:(i + 1) * T], ALU.mult,
                    [("ps", bk), "cs"], [("qtmp", u % 2)])
            qt2 = qtmp2[u % 2]
            self.tt("dve", qt2[64:96, :], ps[bk][96:128, :], cs[96:128, i * T:(i + 1) * T], ALU.mult,
                    [("ps", bk), "cs"], [("qtmp2", u % 2)])
            self.tt("pool", qb[64:96, :], qt[64:96, :], qt2[64:96, :], ALU.add, [("qtmp", u % 2), ("qtmp2", u % 2)],
                    [("qTr", u % 3)])

        kprep(0)
        vprep(0)
        qprep(0)
        total_kc = len(units) * NKC
        bg_acc = 0.0
        att_w = min(NT - 1, ATT_WINDOWS.get(S, 1) - 1)
        bg_rate = min(BG_RATE, ln_end[att_w] / float(total_kc) * 1.03)
        kvq = []
        LA = 3
        NPT = 6
        NBLK = len(units) * NKC

        def s_blk(n):
            u, kc = divmod(n, NKC)
            h, i = units[u]
            b = h % 2
            qb = qT[u % 3]
            bk_ = n % 4
            self.mm(ps[bk_][:, :], kT[b][0:96, kc * 128:(kc + 1) * 128], qb[0:96, :], True, True,
                    [("kTn", b, kc // 4), ("kTr", b, kc // 4), ("qTn", u % 3), ("qTr", u % 3)], [("ps", bk_)])
            self.act(pT[n % NPT][:], ps[bk_][:, :], AF.Exp, [("ps", bk_)], [("pT", n % NPT)], scale=SCALE)

        def pv_blk(n):
            u, kc = divmod(n, NKC)
            h, i = units[u]
            b = h % 2
            ob = 4 + u % 2
            self.mm(ps[ob][:, :], vh[b][:, kc, :], pT[n % NPT][:], kc == 0, kc == NKC - 1,
                    [("vh", b, kc // 8), ("vh1", b), ("pT", n % NPT)], [("ps", ob)])

        def normalize(u):
            h, i = units[u]
            b = h % 2
            ob = 4 + u % 2
            r = rc[u % 2]
            xk = ["cvt_region"] if (si == 0 and h // 2 >= 2) else []
            if S <= 2048 and ACT_RECIP:
                lo, hi = (64, 128) if b == 0 else (0, 64)
                self.act(r[lo:hi, :], ps[ob][lo:hi, :], AF.Ln, [("ps", ob)], [("rc", u % 2)])
                self.act(r[lo:hi, :], r[lo:hi, :], AF.Exp, [("rc", u % 2)], [("rc", u % 2)], scale=-1.0)
                olo, ohi = (0, 64) if b == 0 else (64, 128)
                self.tt("dve", atT[olo:ohi, h // 2, i * T:(i + 1) * T], ps[ob][olo:ohi, :], r[lo:hi, :], ALU.mult,
                        [("ps", ob), ("rc", u % 2)], [("atT", h // 2, b, i)] + xk)
            elif b == 0:
                self.recip(r[64:128, :], ps[ob][64:128, :], [("ps", ob)], [("rc", u % 2)])
                self.tt("dve", atT[0:64, h // 2, i * T:(i + 1) * T], ps[ob][0:64, :], r[64:128, :], ALU.mult,
                        [("ps", ob), ("rc", u % 2)], [("atT", h // 2, 0, i)] + xk)
            else:
                self.recip(r[0:64, :], ps[ob][0:64, :], [("ps", ob)], [("rc", u % 2)])
                self.tt("dve", atT[64:128, h // 2, i * T:(i + 1) * T], ps[ob][64:128, :], r[0:64, :], ALU.mult,
                        [("ps", ob), ("rc", u % 2)], [("atT", h // 2, 1, i)] + xk)

        cvt = self.ffn_convert_items(atT_off + 16384) if si == 0 else []
        cvt_every = 9
        cvt_deadline = 4 * NT * NKC - 64
        for n in range(NBLK + LA):
            if cvt and (n % cvt_every == 0 or n >= cvt_deadline):
                cvt.pop(0)()
                while cvt and n >= cvt_deadline:
                    cvt.pop(0)()
            if n < NBLK:
                s_blk(n)
            m = n - LA
            if m < 0:
                continue
            pv_blk(m)
            u, kc = divmod(m, NKC)
            h, i = units[u]
            bg_acc += bg_rate
            if bg_acc >= 1.0:
                pull(int(bg_acc))
                bg_acc -= int(bg_acc)
            if kc == 3:
                if u + 1 < len(units):
                    qprep(u + 1)
                if i == 0 and h + 1 < NH:
                    kvq = kvprep_items(h + 1)
            if kc >= 4 and kc % 4 == 0 and kvq:
                kvq.pop(0)()
            if kc == NKC - 1:
                if i == min(1, NT - 1):
                    while kvq:
                        kvq.pop(0)()
                normalize(u)
        pull_until(ln_end[att_w])
        P.barrier()

        off = G
        wout, off = self.at("wout", [128, 8, D], BF16, off)
        wdr = []
        for k in range(3):
            t, off = self.at("wdr%d" % k, [128, 2, 512], BF16, off)
            wdr.append(t)
        gu = []
        for k in range(3):
            t, off = self.at("gu%d" % k, [128, 8, 512], BF16, off)
            gu.append(t)
        xts = []
        for k in range(2):
            t, off = self.at("xt%d" % k, [128, 4, D], F32, off)
            xts.append(t)
        x1n, off = self.at("x1n", [128, 4, D], BF16, off)
        x1nT, off = self.at("x1nT", [128, 8, T], BF16, off)
        hT, off = self.at("hT", [128, NFC, T], BF16, off)
        sgb = []
        for k in range(2):
            t, off = self.at("sgb%d" % k, [128, T], F32, off)
            sgb.append(t)
        assert off <= CTOP, off
        sqd = [csq[:, 0, :], csq[:, 1, :]]

        self.dma(wout[:], self.s_wout[:, :, :], reads=["s_wout"], writes=["wout"])
        cnt = {"w": 0, "g": 0, "sq": 0, "gi": 0, "wi": 0}
        XK = lambda i: [("xt", i % 2, j, n) for j in range(4) for n in range(2)]
        XJ = lambda i, j: [("xt", i % 2, j, 0), ("xt", i % 2, j, 1)]

        def d_front(i):
            xt = xts[i % 2]
            so_ = (i % 2) * 0
            self.dma(xt[:], x[i * T:(i + 1) * T, :].rearrange("(j p) d -> p j d", p=128), writes=XK(i))
            pull_until(ln_end[i])
            sb_stat = 6
            for which in range(2):
                for c in range(4):
                    sq = sqd[cnt["sq"] % 2]
                    ksq = ("csq", cnt["sq"] % 2)
                    cnt["sq"] += 1
                    if which == 0:
                        src = atT[:, c, i * T:(i + 1) * T]
                        rk = [("atT", c, 0, i), ("atT", c, 1, i)]
                    else:
                        src = unT[:, c, 16 + i * T:16 + (i + 1) * T]
                        rk = [("un", c, i)]
                    self.act(sq, src, AF.Square, rk, [ksq])
                    for j in range(4):
                        col = (which * 4 + c) * 4 + j
                        self.mm(ps[sb_stat][:, col:col + 1], sq[:, j * 128:(j + 1) * 128], onesf[:, 0:1], True, True,
                                [ksq, "onesf"], [("ps", sb_stat)])
            pv = ps[sb_stat][:, 0:32].rearrange("p (w c j) -> p w j c", w=2, c=4, j=4)
            so = self.stat[:, 16:24].rearrange("p (w j) -> p w j", w=2)
            self.P.op("dve", lambda e, pv=pv, so=so: e.tensor_reduce(out=so, in_=pv, op=ALU.add,
                                                                      axis=mybir.AxisListType.X),
                      reads=[("ps", sb_stat)], writes=[("stat", "d1")])
            self.act(stat[:, 24:32], stat[:, 16:24], AF.Sqrt, [("stat", "d1")], [("stat", "d2")], scale=1.0 / 512, bias=EPS)
            self.recip(stat[:, 32:40], stat[:, 24:32], [("stat", "d2")], [("stat", "d3")])
            for j in range(4):
                for n in range(2):
                    ba = self.bank() % 6
                    bu = self.bank() % 6
                    for c in range(4):
                        self.mm(ps[ba][:, :], atT[:, c, i * T + j * 128:i * T + (j + 1) * 128], wout[:, c, n * 512:(n + 1) * 512],
                                c == 0, c == 3, [("atT", c, 0, i), ("atT", c, 1, i), "wout"], [("ps", ba)])
                    for c in range(4):
                        self.mm(ps[bu][:, :], unT[:, c, 16 + i * T + j * 128:16 + i * T + (j + 1) * 128],
                                wout[:, 4 + c, n * 512:(n + 1) * 512], c == 0, c == 3, [("un", c, i), "wout"], [("ps", bu)])
                    xs_ = xt[:, j, n * 512:(n + 1) * 512]
                    kk = ("xt", i % 2, j, n)
                    self.stt(xs_, ps[ba][:, :], stat[:, 32 + j:33 + j], xs_, ALU.mult, ALU.add,
                             [("ps", ba), ("stat", "d3"), kk], [kk])
                    self.stt(xs_, ps[bu][:, :], stat[:, 36 + j:37 + j], xs_, ALU.mult, ALU.add,
                             [("ps", bu), ("stat", "d3"), kk], [kk])
            for j in range(4):
                self.act(self.junk[:], xt[:, j, :], AF.Square, XJ(i, j), ["junk", ("stat", "e", j)],
                         accum=stat[:, 40 + j:41 + j])
            self.act(stat[:, 44:48], stat[:, 40:44], AF.Sqrt, [("stat", "e", j) for j in range(4)], [("stat", "e2")],
                     scale=1.0 / D, bias=EPS)
            self.recip(stat[:, 48:52], stat[:, 44:48], [("stat", "e2")], [("stat", "e3")])
            for j in range(4):
                sc = stat[:, 48 + j:49 + j]
                if j % 2 == 0:
                    self.ts("dve", x1n[:, j, :], xt[:, j, :], sc, XJ(i, j) + [("stat", "e3")], [("x1n", j)])
                else:
                    self.act(x1n[:, j, :], xt[:, j, :], AF.Copy, XJ(i, j) + [("stat", "e3")], [("x1n", j)], scale=sc)
            for c in range(8):
                b = self.bank() % 6
                pvv = psb[b][:, 0:512]
                kb = ("ps", b)
                for j in range(4):
                    self.tr(pvv[:, j * 128:(j + 1) * 128], x1n[:, j, c * 128:(c + 1) * 128], ident[:],
                            [("x1n", j), "ident"], [kb])
                self.copy("act" if c % 2 == 0 else "dve", x1nT[:, c, :], pvv, [kb], [("x1nT", c)])

        NGQ = NT * 11

        def gu_issue(q):
            if q < NGQ and q >= cnt["gi"]:
                self.dma(gu[q % 3][:], self.s_gu[q % 11, :, :, :], reads=["s_gu"], writes=[("gu", q % 3)])
                cnt["gi"] = q + 1

        NWQ = NT * 22

        def wd_issue(w):
            if w < NWQ and w >= cnt["wi"]:
                fp = w % 11
                n = (w // 11) % 2
                self.dma(wdr[w % 3][:], self.s_wdown[:, 2 * fp:2 * fp + 2, n * 512:(n + 1) * 512], reads=["s_wdown"],
                         writes=[("wdr", w % 3)])
                cnt["wi"] = w + 1

        def d_gateup(i, npull):
            for p in range(11):
                q = i * 11 + p
                for qq in range(q, q + 3):
                    gu_issue(qq)
                g = gu[q % 3]
                kg = ("gu", q % 3)
                for f2 in range(2):
                    f = 2 * p + f2
                    bgt = self.bank() % 6
                    bu = self.bank() % 6
                    for kc in range(8):
                        self.mm(ps[bgt][:, :], g[:, kc, f2 * 128:(f2 + 1) * 128], x1nT[:, kc, :], kc == 0, kc == 7,
                                [kg, ("x1nT", kc)], [("ps", bgt)])
                    for kc in range(8):
                        self.mm(ps[bu][:, :], g[:, kc, 256 + f2 * 128:256 + (f2 + 1) * 128], x1nT[:, kc, :], kc == 0, kc == 7,
                                [kg, ("x1nT", kc)], [("ps", bu)])
                    sg_ = sgb[f % 2]
                    self.act(sg_[:], ps[bgt][:, :], AF.Silu, [("ps", bgt)], [("sgb", f % 2)])
                    self.tt("dve", hT[:, f, :], ps[bu][:, :], sg_[:], ALU.mult, [("ps", bu), ("sgb", f % 2)], [("hT", f)])
                    pull(npull)
                gu_issue(q + 3)
            for ww in range(i * 22, i * 22 + 3):
                wd_issue(ww)

        def d_down(i, npull):
            xt = xts[i % 2]
            for n in range(2):
                bd = [self.bank() % 6 for _ in range(4)]
                for fp in range(11):
                    w = (i * 2 + n) * 11 + fp
                    for ww in range(w, w + 3):
                        wd_issue(ww)
                    wp = wdr[w % 3]
                    kw = ("wdr", w % 3)
                    for f2 in range(2):
                        f = 2 * fp + f2
                        for j in range(4):
                            self.mm(ps[bd[j]][:, :], hT[:, f, j * 128:(j + 1) * 128], wp[:, f2, :],
                                    f == 0, f == NFC - 1, [("hT", f), kw], [("ps", bd[j])])
                    if (w + 3) // 22 == i:
                        wd_issue(w + 3)
                    pull(npull)
                for j in range(4):
                    xs_ = xt[:, j, n * 512:(n + 1) * 512]
                    kk = ("xt", i % 2, j, n)
                    self.tt("dve", xs_, ps[bd[j]][:, :], xs_, ALU.add, [("ps", bd[j]), kk], [kk])

        def d_final(i):
            xt = xts[i % 2]
            for j in range(4):
                self.act(self.junk[:], xt[:, j, :], AF.Square, XJ(i, j), ["junk", ("stat", "f", j)],
                         accum=stat[:, 52 + j:53 + j])
            self.act(stat[:, 56:60], stat[:, 52:56], AF.Sqrt, [("stat", "f", j) for j in range(4)], [("stat", "f2")],
                     scale=1.0 / D, bias=EPS)
            self.recip(stat[:, 60:64], stat[:, 56:60], [("stat", "f2")], [("stat", "f3")])
            for j in range(4):
                self.stt(xt[:, j, :], xt[:, j, :], stat[:, 60 + j:61 + j], self.gfbc[:], ALU.mult, ALU.mult,
                         XJ(i, j) + [("stat", "f3"), "gfbc"], XJ(i, j))
            self.dma(y[i * T:(i + 1) * T, :].rearrange("(j p) d -> p j d", p=128), xt[:], reads=XK(i))

        d_front(0)
        for i in range(NT):
            d_gateup(i, BG_GU)
            if i + 1 < NT:
                d_front(i + 1)
            d_down(i, BG_DN)
            d_final(i)
        P.barrier()

    def build(self):
        self.consts()
        self.prologue()
        for si in range(2):
            self.sequence(si)
        self.P.emit()
        return self.nc


def _rope_table():
    S = S_LIST[0]
    inv = (1.0 / (np.float32(10000.0) ** (np.arange(0, 32, 2, dtype=np.float32) / np.float32(32)))).astype(np.float32)
    ang = (np.arange(S, dtype=np.float32)[:, None] * inv[None, :]).astype(np.float32)
    cos = np.cos(ang).astype(np.float32).T
    sin = np.sin(ang).astype(np.float32).T
    tab = np.ones((128, S), np.float32)
    tab[64:80] = cos
    tab[80:96] = cos
    tab[96:112] = sin
    tab[112:128] = sin
    return tab


_NC_CACHE = {}


def kernel(x_prompt, x_sample, norm1_g, w_in, q_norm_g, w_uq, kv_norm_g, w_ukv,
           dw_w, dw_b, conv_ln_g, conv_ln_b, attn_out_g, conv_out_g, w_out,
           norm2_g, w_gate, w_up, w_down, final_g):
    f = lambda a: np.ascontiguousarray(np.asarray(a, dtype=np.float32))
    shared = {
        "norm1_g": f(norm1_g)[0], "w_in": f(w_in)[0], "q_norm_g": f(q_norm_g)[0], "w_uq": f(w_uq)[0],
        "kv_norm_g": f(kv_norm_g)[0], "w_ukv": f(w_ukv)[0], "dw_w": f(dw_w)[0], "dw_b": f(dw_b)[0],
        "conv_ln_g": f(conv_ln_g)[0], "conv_ln_b": f(conv_ln_b)[0], "attn_out_g": f(attn_out_g)[0],
        "conv_out_g": f(conv_out_g)[0], "w_out": f(w_out)[0], "norm2_g": f(norm2_g)[0],
        "w_gate": f(w_gate)[0], "w_up": f(w_up)[0], "w_down": f(w_down)[0], "final_g": f(final_g),
        "rope_cs": _rope_table(),
    }
    xp = f(x_prompt)
    xs = f(x_sample)
    if "nc" not in _NC_CACHE:
        _NC_CACHE["nc"] = Builder().build()
    nc = _NC_CACHE["nc"]
    in_maps = []
    for b in range(NCORES):
        m = dict(shared)
        m["xp"] = xp[b]
        m["xs"] = xs[b]
        in_maps.append(m)
    res = run_bass_kernel_spmd(nc, in_maps, core_ids=list(range(NCORES)))
    yp = np.stack([np.asarray(res.results[b]["yp"], dtype=np.float32) for b in range(NCORES)], axis=0)
    ys = np.stack([np.asarray(res.results[b]["ys"], dtype=np.float32) for b in range(NCORES)], axis=0)
    return (yp, ys)
```
